# Optimizing a Trainium2 kernel written in Bass

```python
import math
import jax, jax.numpy as jnp
from jax import lax
import numpy as np

D_MODEL = 2048
BATCH = 16
SEQ = 2048
DEPTH = 2
DEC_BATCH = 2
DEC_SEQ = 16384
PAST_LEN = 128

N_MIXERS = 2
N_A_LAYERS = (DEPTH + 1) // 2
N_B_LAYERS = DEPTH // 2
CHUNK = 128
GMLP_WIDTH = D_MODEL
GMLP_GROUPS = 16
GMLP_GROUP_DIM = GMLP_WIDTH // GMLP_GROUPS
HEAD_DIM = 128
HEADS_PER_GROUP = 8
DILATION_PAIRS = ((128, 1), (512, 4), (2048, 16))
N_DIL_GROUPS = len(DILATION_PAIRS)
ATTN_WIDTH = HEADS_PER_GROUP * HEAD_DIM
ROT_DIM = HEAD_DIM // 4
ROPE_THETA = 500000.0
NEG_BIG = -1e30
D_FF = ((8 * D_MODEL + 3 * 256 - 1) // (3 * 256)) * 256
PLE_DIM = 256
DEEPNORM_ALPHA = (2.0 * DEPTH) ** 0.25
DEEPNORM_BETA = (8.0 * DEPTH) ** -0.25
LN_EPS = 1e-5

kernel_name = "hybrid_gmlp_dilated_attn_encoder"


def layer_norm(x, g, b):
    xf = x.astype(jnp.float32)
    mu = jnp.mean(xf, axis=-1, keepdims=True)
    var = jnp.mean(jnp.square(xf - mu), axis=-1, keepdims=True)
    y = (xf - mu) * lax.rsqrt(var + LN_EPS)
    return (y * g.astype(jnp.float32) + b.astype(jnp.float32)).astype(x.dtype)


def rotary_partial(t, pos):
    half = ROT_DIM // 2
    inv_freq = ROPE_THETA ** (-jnp.arange(0, ROT_DIM, 2, dtype=jnp.float32) / ROT_DIM)
    ang = pos.astype(jnp.float32)[:, None] * inv_freq[None, :]
    cos = jnp.cos(ang)[None, :, None, :]
    sin = jnp.sin(ang)[None, :, None, :]
    tr = t[..., :ROT_DIM].astype(jnp.float32)
    t1, t2 = tr[..., :half], tr[..., half:]
    rot = jnp.concatenate([t1 * cos - t2 * sin, t1 * sin + t2 * cos], axis=-1)
    return jnp.concatenate([rot.astype(t.dtype), t[..., ROT_DIM:]], axis=-1)


def banded_attention(q, k, v, half):
    bp, L, H, dh = q.shape
    blk = 2 * half
    nb = -(-L // blk)
    lp = nb * blk
    qp = jnp.pad(q, ((0, 0), (0, lp - L), (0, 0), (0, 0))).reshape(bp, nb, blk, H, dh)
    pad_k = ((0, 0), (half, lp - L + half), (0, 0), (0, 0))
    kp = jnp.pad(k, pad_k).reshape(bp, nb + 1, blk, H, dh)
    vp = jnp.pad(v, pad_k).reshape(bp, nb + 1, blk, H, dh)
    kwin = jnp.concatenate([kp[:, :-1], kp[:, 1:]], axis=2)
    vwin = jnp.concatenate([vp[:, :-1], vp[:, 1:]], axis=2)
    s = jnp.einsum('bnqhd,bnkhd->bnhqk', qp.astype(jnp.float32), kwin.astype(jnp.float32))
    a = jnp.arange(blk)[:, None]
    c = jnp.arange(2 * blk)[None, :]
    rel = c - a
    band = (rel >= 0) & (rel <= 2 * half)
    j = jnp.arange(nb)[:, None, None] * blk + c[None] - half
    mask = band[None] & (j >= 0) & (j < L)
    s = jnp.where(mask[None, :, None], s, NEG_BIG)
    m = jnp.max(s, axis=-1, keepdims=True)
    p = jnp.exp(s - m)
    den = jnp.sum(p, axis=-1)
    o = jnp.einsum('bnhqk,bnkhd->bnqhd', p, vwin.astype(jnp.float32))
    o = o / jnp.transpose(den, (0, 1, 3, 2))[..., None]
    lse = jnp.transpose(m[..., 0] + jnp.log(den), (0, 1, 3, 2))
    o = o.reshape(bp, lp, H, dh)[:, :L]
    lse = lse.reshape(bp, lp, H)[:, :L]
    return o, lse


def dilated_group(q, k, v, dil, half):
    b, s, h, dh = q.shape
    L = s // dil

    def to_sub(t):
        return jnp.transpose(t.reshape(b, L, dil, h, dh), (0, 2, 1, 3, 4)).reshape(b * dil, L, h, dh)

    o, lse = banded_attention(to_sub(q), to_sub(k), to_sub(v), half)
    o = jnp.transpose(o.reshape(b, dil, L, h, dh), (0, 2, 1, 3, 4)).reshape(b, s, h, dh)
    lse = jnp.transpose(lse.reshape(b, dil, L, h), (0, 2, 1, 3)).reshape(b, s, h)
    return o, lse


def dilated_attention(x, w_qkv, w_o, pos):
    b, s, _ = x.shape
    qkv = (x @ w_qkv).reshape(b, s, N_DIL_GROUPS, 3, HEADS_PER_GROUP, HEAD_DIM)
    outs, lses = [], []
    for g, (window, dil) in enumerate(DILATION_PAIRS):
        q = rotary_partial(qkv[:, :, g, 0], pos) * (HEAD_DIM ** -0.5)
        k = rotary_partial(qkv[:, :, g, 1], pos)
        v = qkv[:, :, g, 2]
        o, lse = dilated_group(q, k, v, dil, window // (2 * dil))
        outs.append(o)
        lses.append(lse)
    wts = jax.nn.softmax(jnp.stack(lses, axis=0), axis=0)
    o = sum(wts[g][..., None] * outs[g] for g in range(N_DIL_GROUPS))
    return o.astype(x.dtype).reshape(b, s, ATTN_WIDTH) @ w_o


def chunked_gmlp(x, w_in, ln_g, ln_b, w_s, b_s, w_o):
    b, s, _ = x.shape
    h = jax.nn.gelu(x @ w_in, approximate=False)
    u, v = jnp.split(h, 2, axis=-1)
    v = layer_norm(v, ln_g, ln_b).reshape(b, s // CHUNK, CHUNK, GMLP_GROUPS, GMLP_GROUP_DIM)
    v = jnp.einsum('gpq,bnqgc->bnpgc', w_s, v) + jnp.transpose(b_s)[:, :, None]
    return (u * v.reshape(b, s, GMLP_WIDTH)) @ w_o


def swiglu(x, w_gu, w_down):
    g, u = jnp.split(x @ w_gu, 2, axis=-1)
    return (jax.nn.silu(g) * u) @ w_down


def trunk(x, p, w_in_a, ln_v_g, ln_v_b, w_s_a, b_s_a, w_o_a, w_qkv_b, w_o_b,
          ln_mix_g, ln_mix_b, w_ffn_gu, w_ffn_down, ln_ffn_g, ln_ffn_b,
          w_ple_gate, w_ple_proj):
    pos = jnp.arange(x.shape[1], dtype=jnp.float32)
    for i in range(DEPTH):
        j = i // N_MIXERS
        if i % N_MIXERS == 0:
            mix = chunked_gmlp(x, w_in_a[j], ln_v_g[j], ln_v_b[j], w_s_a[j], b_s_a[j], w_o_a[j])
        else:
            mix = dilated_attention(x, w_qkv_b[j], w_o_b[j], pos)
        x = layer_norm(DEEPNORM_ALPHA * x + mix, ln_mix_g[i], ln_mix_b[i])
        x = layer_norm(DEEPNORM_ALPHA * x + swiglu(x, w_ffn_gu[i], w_ffn_down[i]), ln_ffn_g[i], ln_ffn_b[i])
        x = x + jax.nn.sigmoid(x @ w_ple_gate[i]) * (p[i] @ w_ple_proj[i])
    return x


def setup_inputs(seed: int = 0) -> dict:
    key = jax.random.key(seed)
    ks = jax.random.split(key, 24)
    f32 = jnp.float32

    def nrm(k, shape, scale):
        return jax.random.normal(k, shape, f32) * scale

    return {
        "x_prompt": nrm(ks[0], (BATCH, SEQ, D_MODEL), 1.0),
        "x_sample": nrm(ks[1], (DEC_BATCH, DEC_SEQ, D_MODEL), 1.0),
        "p_prompt": nrm(ks[2], (DEPTH, BATCH, SEQ, PLE_DIM), 1.0),
        "p_sample": nrm(ks[3], (DEPTH, DEC_BATCH, DEC_SEQ, PLE_DIM), 1.0),
        "w_in_a": nrm(ks[4], (N_A_LAYERS, D_MODEL, 2 * GMLP_WIDTH), D_MODEL ** -0.5),
        "ln_v_g": 1.0 + nrm(ks[5], (N_A_LAYERS, GMLP_WIDTH), 0.02),
        "ln_v_b": nrm(ks[6], (N_A_LAYERS, GMLP_WIDTH), 0.02),
        "w_s_a": nrm(ks[7], (N_A_LAYERS, GMLP_GROUPS, CHUNK, CHUNK), CHUNK ** -0.5),
        "b_s_a": 1.0 + nrm(ks[8], (N_A_LAYERS, GMLP_GROUPS, CHUNK), 0.01),
        "w_o_a": nrm(ks[9], (N_A_LAYERS, GMLP_WIDTH, D_MODEL), DEEPNORM_BETA * GMLP_WIDTH ** -0.5),
        "w_qkv_b": nrm(ks[10], (N_B_LAYERS, D_MODEL, N_DIL_GROUPS * 3 * ATTN_WIDTH), D_MODEL ** -0.5),
        "w_o_b": nrm(ks[11], (N_B_LAYERS, ATTN_WIDTH, D_MODEL), DEEPNORM_BETA * ATTN_WIDTH ** -0.5),
        "ln_mix_g": 1.0 + nrm(ks[12], (DEPTH, D_MODEL), 0.02),
        "ln_mix_b": nrm(ks[13], (DEPTH, D_MODEL), 0.02),
        "w_ffn_gu": nrm(ks[14], (DEPTH, D_MODEL, 2 * D_FF), D_MODEL ** -0.5),
        "w_ffn_down": nrm(ks[15], (DEPTH, D_FF, D_MODEL), DEEPNORM_BETA * D_FF ** -0.5),
        "ln_ffn_g": 1.0 + nrm(ks[16], (DEPTH, D_MODEL), 0.02),
        "ln_ffn_b": nrm(ks[17], (DEPTH, D_MODEL), 0.02),
        "w_ple_gate": nrm(ks[18], (DEPTH, D_MODEL, D_MODEL), D_MODEL ** -0.5),
        "w_ple_proj": nrm(ks[19], (DEPTH, PLE_DIM, D_MODEL), 0.5 * PLE_DIM ** -0.5),
    }


def reference(x_prompt, x_sample, p_prompt, p_sample, w_in_a, ln_v_g, ln_v_b, w_s_a, b_s_a,
              w_o_a, w_qkv_b, w_o_b, ln_mix_g, ln_mix_b, w_ffn_gu, w_ffn_down, ln_ffn_g,
              ln_ffn_b, w_ple_gate, w_ple_proj):
    y_prompt = trunk(x_prompt, p_prompt, w_in_a, ln_v_g, ln_v_b, w_s_a, b_s_a, w_o_a, w_qkv_b,
                     w_o_b, ln_mix_g, ln_mix_b, w_ffn_gu, w_ffn_down, ln_ffn_g, ln_ffn_b,
                     w_ple_gate, w_ple_proj)
    y_sample = trunk(x_sample, p_sample, w_in_a, ln_v_g, ln_v_b, w_s_a, b_s_a, w_o_a, w_qkv_b,
                     w_o_b, ln_mix_g, ln_mix_b, w_ffn_gu, w_ffn_down, ln_ffn_g, ln_ffn_b,
                     w_ple_gate, w_ple_proj)
    return (y_prompt, y_sample)
```

```python
import math
import numpy as np
import concourse.bass as bass
import concourse.mybir as mybir
from concourse.bass_utils import run_bass_kernel_spmd

F32 = mybir.dt.float32
BF16 = mybir.dt.bfloat16
AF = mybir.ActivationFunctionType
ALU = mybir.AluOpType

D = 2048
DFF = 5632
NCORE = 8
ALPHA = 4.0 ** 0.25
EPS = 1e-5
DILS = (1, 4, 16)
NKT = (17, 5, 2)
KOFF = (0, 17, 37)
NKB = 69
QSCALE = 128.0 ** -0.5
NEG = -30000.0
NOWN = 8192
NHALO = 1024
NLOC = NOWN + NHALO
EXT = NLOC + NHALO
TILE_E0 = [1024 + 512 * i for i in range(16)] + [9216, 9728]
TILE_E0C = [-1] * 16 + [0, 512]
TILE_OWN = [i for i in range(16)] + [-1, -1]
ST_E0 = [1024 + 2048 * i for i in range(4)]

ENGS = ("pe", "act", "dve", "pool", "sp")
EPOCH = 24000


class Res:
    __slots__ = ("writer", "readers", "name")

    def __init__(self, name=""):
        self.writer = None
        self.readers = []
        self.name = name


class Op:
    __slots__ = ("eng", "fn", "deps", "dma", "signal", "ticket", "dsem", "dval", "idx", "prev_dval")

    def __init__(self, eng, fn, dma):
        self.eng = eng
        self.fn = fn
        self.dma = dma
        self.deps = []
        self.signal = False
        self.ticket = 0
        self.dsem = -1
        self.dval = 0
        self.prev_dval = 0
        self.idx = 0


class Prog:
    def __init__(self):
        self.q = {e: [] for e in ENGS}
        self.dma_ops = {"sp": [], "pool": []}
        self.barrier_deps = {e: [] for e in ENGS}
        self.all_dma_since = []

    def op(self, eng, fn, reads=(), writes=(), dma=False, extra=()):
        o = Op(eng, fn, dma)
        deps = {}
        for d in extra:
            deps[id(d)] = d
        for r in reads:
            if r.writer is not None:
                deps[id(r.writer)] = r.writer
        for w in writes:
            if w.writer is not None:
                deps[id(w.writer)] = w.writer
            for rd in w.readers:
                deps[id(rd)] = rd
        if self.barrier_deps[eng]:
            for d in self.barrier_deps[eng]:
                deps[id(d)] = d
            self.barrier_deps[eng] = []
        o.deps = [d for d in deps.values() if d is not o]
        for r in reads:
            r.readers.append(o)
        for w in writes:
            w.writer = o
            w.readers = []
        o.idx = len(self.q[eng])
        self.q[eng].append(o)
        if dma:
            self.dma_ops[eng].append(o)
            self.all_dma_since.append(o)
        return o

    def barrier(self):
        lasts = []
        for e in ENGS:
            for o in reversed(self.q[e]):
                if not o.dma:
                    lasts.append(o)
                    break
        alld = lasts + self.all_dma_since
        self.all_dma_since = []
        for e in ENGS:
            self.barrier_deps[e] = list(alld)


def build(tile_list=tuple(range(18)), st_list=(0, 1, 2, 3), debug=False, stop=99):
    nc = bass.Bass("TRN2", target_bir_lowering=False)
    P = Prog()

    def din(name, shape, dt=F32):
        return nc.dram_tensor(name, list(shape), dt, kind="ExternalInput").ap()

    def dscr(name, shape, dt):
        return nc.dram_tensor(name, list(shape), dt, kind="Internal").ap()

    xin = din("xin", [NLOC, D])
    p0in = din("p0in", [NLOC, 256])
    p1in = din("p1in", [NOWN, 256])
    posin = din("posin", [NLOC, 1])
    kbin = din("kbin", [128, 4 * NKB])
    cident = din("cident", [128, 128])
    cmask = din("cmask", [128, 256])
    cinvf = din("cinvf", [16])
    Wf = {
        "w_in": din("w_in", [D, 4096]),
        "w_o_a": din("w_o_a", [D, D]),
        "w_qkv": din("w_qkv", [D, 9216]),
        "w_o_b": din("w_o_b", [1024, D]),
        "gu0": din("gu0", [D, 2 * DFF]), "gu1": din("gu1", [D, 2 * DFF]),
        "dn0": din("dn0", [DFF, D]), "dn1": din("dn1", [DFF, D]),
        "gt0": din("gt0", [D, D]), "gt1": din("gt1", [D, D]),
        "pj0": din("pj0", [256, D]), "pj1": din("pj1", [256, D]),
    }
    w_s = din("w_s", [16, 128, 128])
    b_s = din("b_s", [2048])
    lnv = din("ln_v", [2, D])
    lnmix = din("ln_mix", [2, 2, D])
    lnffn = din("ln_ffn", [2, 2, D])
    yout = nc.dram_tensor("yout", [NOWN, D], F32, kind="ExternalOutput").ap()
    if debug:
        x3s = nc.dram_tensor("x3s", [NOWN, D], F32, kind="ExternalOutput").ap()
    else:
        x3s = dscr("x3s", [NOWN, D], F32)
    Wb = {k: dscr(k + "_b", v.shape, BF16) for k, v in Wf.items()}
    KT = dscr("KT", [24, 128, EXT], BF16)
    QT = dscr("QT", [24, 128, EXT], BF16)
    Vs = dscr("Vs", [EXT, 3072], BF16)
    Wres = {k: [Res(f"{k}{i}") for i in range(v.shape[0] // 128)] for k, v in Wf.items()}
    KTres, QTres, Vres, x3res = Res("KT"), Res("QT"), Res("Vs"), Res("x3s")

    BASE = 16640
    LIMIT = 229376
    cur = [BASE]

    def sb(name, shape, dt, at=None):
        nbytes = int(np.prod(shape[1:])) * (2 if dt == BF16 else 4)
        nbytes = (nbytes + 63) // 64 * 64
        if at is None:
            off = cur[0]
            cur[0] += nbytes
        else:
            off = at
        assert off + nbytes <= LIMIT, (name, off, nbytes)
        t = nc.alloc_sbuf_tensor_at(name, list(shape), dt, offset=off)
        return t

    ident = sb("ident", [128, 128], BF16)
    ones = sb("ones", [128, 128], BF16)
    mask2 = sb("mask2", [128, 256], BF16)
    wsT = sb("wsT", [128, 16, 128], BF16)
    bsb = sb("bsb", [128, 16, 128], F32)
    kb = sb("kb", [128, 4 * NKB], F32)
    invf = sb("invf", [128, 16], F32)
    stat = sb("stat", [128, 4, 4, 6], F32)
    mv = sb("mv", [128, 4, 2], F32)
    rstd = sb("rstd", [128, 4, 1], F32)
    epsc = sb("epsc", [128, 1], F32)
    gb_off = cur[0]
    gb = sb("gb", [128, 2, D], F32)
    NSLOT = 3
    ring_off0 = cur[0]
    ring = [sb(f"ring{i}", [128, 16, 512], BF16) for i in range(NSLOT)]
    ring_res = [Res(f"ring{i}") for i in range(NSLOT)]
    PH = cur[0]

    cur[0] = PH
    attnT = sb("attnT", [128, 8, 2048], BF16)
    PH2 = cur[0]

    def tile_bufs(base, tag, with_p1):
        cur[0] = base
        d = {}
        d["xs"] = [sb(f"xs{tag}{s}", [128, D], F32) for s in range(4)]
        d["xb"] = [sb(f"xb{tag}{i}", [128, D], BF16) for i in range(2)]
        d["xT"] = sb(f"xT{tag}", [128, 16, 512], BF16)
        d["scrA"] = sb(f"scrA{tag}", [128, 24, 512], BF16)
        d["tmpf"] = [sb(f"tmpf{tag}{i}", [128, 512], F32) for i in range(2)]
        d["pin"] = [sb(f"pin{tag}{s}", [128, 256], F32) for s in range(4)]
        d["pb"] = [sb(f"pb{tag}{s}", [128, 256], BF16) for s in range(4)]
        d["pT"] = sb(f"pT{tag}", [128, 2, 512], BF16)
        if with_p1:
            vv_off = cur[0]
            d["vv"] = [sb(f"vv{tag}{s}", [128, D], BF16) for s in range(4)]
            d["qktm"] = [sb(f"qktm{tag}{i}", [128, 4, 512], BF16, at=vv_off + i * 4096) for i in range(2)]
            d["qst"] = [sb(f"qst{tag}{i}", [128, 4, 512], BF16, at=vv_off + 8192 + i * 4096) for i in range(2)]
            ring.append(sb("ring3", [128, 16, 512], BF16))
            ring_res.append(Res("ring3"))
            d["vst"] = [sb(f"vst{tag}{i}", [128, 512], BF16) for i in range(2)]
            d["rot"] = sb(f"rot{tag}", [128, 4, 4, 16], F32)
            d["rtmp"] = sb(f"rtmp{tag}", [128, 2, 4, 16], F32)
            d["ang"] = sb(f"ang{tag}", [128, 4, 16], F32)
            d["posb"] = sb(f"posb{tag}", [128, 4, 1], F32)
        d["xs_r"] = [Res(f"xs{s}") for s in range(4)]
        d["xb_r"] = [Res("xb0"), Res("xb1")]
        d["xT_r"] = [Res(f"xT{s}") for s in range(4)]
        d["scrA_r"] = [Res(f"scrA{j}") for j in range(24)]
        d["tmpf_r"] = [Res("tmpf0"), Res("tmpf1")]
        d["pin_r"] = [Res(f"pin{s}") for s in range(4)]
        d["pb_r"] = [Res(f"pb{s}") for s in range(4)]
        d["pT_r"] = [Res(f"pT{s}") for s in range(4)]
        d["qktm_r"] = [Res("qk0"), Res("qk1")]
        d["qst_r"] = [Res("qst0"), Res("qst1")]
        d["vv_r"] = d["qktm_r"] + d["qst_r"]
        d["vst_r"] = [Res("vst0"), Res("vst1")]
        d["rot_r"] = [Res(f"rot{s}") for s in range(4)]
        d["rtmp_r"] = Res("rtmp")
        d["ang_r"] = Res("ang")
        d["posb_r"] = [Res(f"posb{s}") for s in range(4)]
        d["cnt"] = {"tmpf": 0, "xb": 0, "qk": 0, "qst": 0, "vst": 0}
        return d

    B1 = tile_bufs(PH, "a", True)
    B2 = tile_bufs(PH2, "b", False)
    cur[0] = PH2
    acc = sb("acc", [128, 2, 2, 2048], F32)
    qwin = [sb(f"qwin{i}", [128, 2048], BF16) for i in range(2)]
    kwin = [sb(f"kwin{i}", [128, 4096], BF16) for i in range(2)]
    vwin = [sb(f"vwin{i}", [128, 32, 256], BF16) for i in range(2)]
    ptf = [sb(f"ptf{i}", [128, 256], F32) for i in range(4)]
    ptm = [sb(f"ptm{i}", [128, 256], BF16) for i in range(4)]
    acc_r = [Res("acc0"), Res("acc1")]
    qk_r = [Res("qkw0"), Res("qkw1")]
    kw_r = [Res("kw0"), Res("kw1")]
    vwin_r = [Res("vw0"), Res("vw1")]
    ptf_r = [Res(f"ptf{i}") for i in range(4)]
    ptm_r = [Res(f"ptm{i}") for i in range(4)]
    attnT_r = [Res(f"attnT{h}") for h in range(8)]
    gb_r = Res("gb")
    stat_r = [Res(f"stat{s}") for s in range(4)]
    rstd_r = Res("rstd")
    stat_rc = [[Res(f"statc{s}{c}") for c in range(4)] for s in range(4)]

    banks = [nc.alloc_psum_tensor(f"ps{i}", [128, 512], F32) for i in range(8)]
    bank_r = [Res(f"bank{i}") for i in range(8)]
    bcnt = [0]

    def nbank():
        i = bcnt[0] % 8
        bcnt[0] += 1
        return banks[i], bank_r[i]

    def mm_group(bres, items, reads, extra=(), cont=False):
        n = len(items)
        first = last = None
        for i, (o_, l_, r_, st_, sp_) in enumerate(items):
            def fn(e, o_=o_, l_=l_, r_=r_, st_=st_, sp_=sp_):
                return e.matmul(o_, l_, r_, start=st_, stop=sp_)
            op = P.op("pe", fn, reads=reads if i == 0 else (), writes=[bres] if (i == 0 and not cont) else (),
                      extra=extra if i == 0 else ())
            if i == 0:
                first = op
            last = op
        if last is not first or cont:
            for r in reads:
                r.readers.append(last)
            bres.writer = last
        return last

    def tr_group(bres, items, reads):
        first = last = None
        for i, (o_, in_) in enumerate(items):
            def fn(e, o_=o_, in_=in_):
                return e.transpose(o_, in_, ident[:, :])
            op = P.op("pe", fn, reads=reads if i == 0 else (), writes=[bres] if i == 0 else ())
            if i == 0:
                first = op
            last = op
        if last is not first:
            for r in reads:
                r.readers.append(last)
            bres.writer = last
        return last

    slot_cnt = [0]
    nslot_cur = [4]

    def load_w(wname, kc0, nk, c0, ncols=512):
        i = slot_cnt[0] % nslot_cur[0]
        slot_cnt[0] += 1
        src = Wb[wname][kc0 * 128:(kc0 + nk) * 128, c0:c0 + ncols].rearrange("(k p) c -> p k c", p=128)
        dst = ring[i][:, 0:nk, 0:ncols]
        P.op("sp", lambda e, dst=dst, src=src: e.dma_start(out=dst, in_=src),
             reads=[Wres[wname][kc0 + k] for k in range(nk)], writes=[ring_res[i]], dma=True)
        return ring[i], ring_res[i]

    gb_extra = [None]

    def ln_stats(bufs, stats_done):
        for s, (buf_ap, buf_res) in enumerate(bufs):
            if not stats_done:
                for c in range(4):
                    P.op("dve", lambda e, c=c, s=s, buf_ap=buf_ap: e.bn_stats(stat[:, s, c, :], buf_ap[:, c * 512:(c + 1) * 512]),
                         reads=[buf_res], writes=[stat_rc[s][c]])
            P.op("dve", lambda e, s=s: e.bn_aggr(mv[:, s, :], stat[:, s, :, :].rearrange("p a b -> p (a b)")),
                 reads=stat_rc[s], writes=[stat_r[s]])
        P.op("act", lambda e: e.activation(rstd[:, :, :], mv[:, :, 1:2], AF.Sqrt, bias=epsc[:, 0:1]),
             reads=stat_r, writes=[rstd_r])
        P.op("dve", lambda e: e.reciprocal(rstd[:, :, :], rstd[:, :, :]), reads=[rstd_r], writes=[rstd_r])

    def ln_apply(s, buf_ap, buf_res):
        rd = [buf_res, stat_r[s], rstd_r, gb_r]
        if gb_extra[0] is not None:
            rd = rd + [gb_extra[0]]
        P.op("dve", lambda e: e.scalar_tensor_tensor(buf_ap, buf_ap, mv[:, s, 0:1], gb[:, 0, :], ALU.subtract, ALU.mult),
             reads=rd, writes=[buf_res])
        P.op("dve", lambda e: e.scalar_tensor_tensor(buf_ap, buf_ap, rstd[:, s, :], gb[:, 1, :], ALU.mult, ALU.add),
             reads=rd, writes=[buf_res])

    def make_xT(Bf, s, ev1="dve"):
        i = Bf["cnt"]["xb"] % 2
        Bf["cnt"]["xb"] += 1
        xb, xbr = Bf["xb"][i], Bf["xb_r"][i]
        xs, xsr = Bf["xs"][s], Bf["xs_r"][s]
        P.op("act", lambda e: e.copy(xb[:, :], xs[:, :]), reads=[xsr], writes=[xbr])
        for half in range(2):
            bk, br = nbank()
            bv = bk[:, 0:512].bitcast(BF16)
            items = [(bv[:, k * 128:(k + 1) * 128], xb[:, (half * 8 + k) * 128:(half * 8 + k + 1) * 128]) for k in range(8)]
            tr_group(br, items, [xbr])
            dst = Bf["xT"][:, half * 8:(half + 1) * 8, s * 128:(s + 1) * 128]
            srcv = bv.rearrange("p (k t) -> p k t", t=128)
            if half == 0 or ev1 == "act":
                P.op("act", lambda e, dst=dst, srcv=srcv: e.copy(dst, srcv), reads=[br], writes=[Bf["xT_r"][s]])
            else:
                P.op("dve", lambda e, dst=dst, srcv=srcv: e.tensor_copy(dst, srcv), reads=[br], writes=[Bf["xT_r"][s]])

    def next_tmp(Bf):
        i = Bf["cnt"]["tmpf"] % 2
        Bf["cnt"]["tmpf"] += 1
        return Bf["tmpf"][i], Bf["tmpf_r"][i]

    def ffn(Bf, layer):
        gu, dn = f"gu{layer}", f"dn{layer}"
        xT, xTr = Bf["xT"], Bf["xT_r"]
        scrA, scrAr = Bf["scrA"], Bf["scrA_r"]
        for hf, (blk0, nblk) in enumerate(((0, 6), (6, 5))):
            j0 = blk0 * 4
            for blk in range(blk0, blk0 + nblk):
                sg, sgr = load_w(gu, 0, 16, blk * 512)
                su, sur = load_w(gu, 0, 16, DFF + blk * 512)
                for jj in range(4):
                    jl = blk * 4 + jj - j0
                    bg, bgr = nbank()
                    mm_group(bgr, [(bg[:, :], sg[:, kc, jj * 128:(jj + 1) * 128], xT[:, kc, :], kc == 0, kc == 15)
                                   for kc in range(16)], [sgr] + xTr)
                    bu, bur = nbank()
                    mm_group(bur, [(bu[:, :], su[:, kc, jj * 128:(jj + 1) * 128], xT[:, kc, :], kc == 0, kc == 15)
                                   for kc in range(16)], [sur] + xTr)
                    tmp, tmpr = next_tmp(Bf)
                    P.op("act", lambda e, tmp=tmp, bg=bg: e.activation(tmp[:, :], bg[:, :], AF.Silu),
                         reads=[bgr], writes=[tmpr])
                    P.op("dve", lambda e, jl=jl, tmp=tmp, bu=bu: e.tensor_tensor(scrA[:, jl, :], tmp[:, :], bu[:, :], ALU.mult),
                         reads=[tmpr, bur], writes=[scrAr[jl]])
            nk = nblk * 4
            h1 = nk // 2
            if hf == 1:
                load_gb_pair(lnffn[layer])
            for b in range(4):
                s1, s1r = load_w(dn, j0, h1, b * 512)
                s2, s2r = load_w(dn, j0 + h1, nk - h1, b * 512)
                bks = [nbank() for _ in range(4)]
                for part, (slot, slr_, lo, hi) in enumerate(((s1, s1r, 0, h1), (s2, s2r, h1, nk))):
                    for s in range(4):
                        bk, br = bks[s]
                        items = [(bk[:, :], scrA[:, jl, s * 128:(s + 1) * 128], slot[:, jl - lo, :], jl == 0, jl == nk - 1)
                                 for jl in range(lo, hi)]
                        mm_group(br, items, [slr_] + scrAr[lo:hi], cont=(part == 1))
                for s in range(4):
                    bk, br = bks[s]
                    xsl = Bf["xs"][s][:, b * 512:(b + 1) * 512]
                    if hf == 0:
                        P.op("dve", lambda e, xsl=xsl, bk=bk: e.scalar_tensor_tensor(xsl, xsl, ALPHA, bk[:, :], ALU.mult, ALU.add),
                             reads=[br], writes=[Bf["xs_r"][s]])
                    else:
                        P.op("dve", lambda e, xsl=xsl, bk=bk: e.tensor_tensor(xsl, xsl, bk[:, :], ALU.add),
                             reads=[br], writes=[Bf["xs_r"][s]])
                        P.op("dve", lambda e, xsl=xsl, s=s, b=b: e.bn_stats(stat[:, s, b, :], xsl),
                             reads=[Bf["xs_r"][s]], writes=[stat_rc[s][b]])

    def ple(Bf, layer, pin_dram, prow0):
        gt, pj = f"gt{layer}", f"pj{layer}"
        xT, xTr = Bf["xT"], Bf["xT_r"]
        for s in range(4):
            pin, pinr = Bf["pin"][s], Bf["pin_r"][s]
            src = pin_dram[prow0 + s * 128: prow0 + (s + 1) * 128, :]
            P.op("sp", lambda e, pin=pin, src=src: e.dma_start(out=pin[:, :], in_=src), writes=[pinr], dma=True)
            pb, pbr = Bf["pb"][s], Bf["pb_r"][s]
            P.op("pool", lambda e, pb=pb, pin=pin: e.tensor_copy(pb[:, :], pin[:, :]), reads=[pinr], writes=[pbr])
            bk, br = nbank()
            bv = bk[:, 0:512].bitcast(BF16)
            tr_group(br, [(bv[:, k * 128:(k + 1) * 128], pb[:, k * 128:(k + 1) * 128]) for k in range(2)], [pbr])
            dst = Bf["pT"][:, :, s * 128:(s + 1) * 128]
            srcv = bv[:, 0:256].rearrange("p (k t) -> p k t", t=128)
            P.op("act", lambda e, dst=dst, srcv=srcv: e.copy(dst, srcv), reads=[br], writes=[Bf["pT_r"][s]])
        for b in range(4):
            sg, sgr = load_w(gt, 0, 16, b * 512)
            sp_, spr = load_w(pj, 0, 2, b * 512)
            for s in range(4):
                bg, bgr = nbank()
                mm_group(bgr, [(bg[:, :], xT[:, kc, s * 128:(s + 1) * 128], sg[:, kc, :], kc == 0, kc == 15)
                               for kc in range(16)], [sgr, xTr[s]])
                bp, bpr = nbank()
                mm_group(bpr, [(bp[:, :], Bf["pT"][:, kc, s * 128:(s + 1) * 128], sp_[:, kc, :], kc == 0, kc == 1)
                               for kc in range(2)], [spr, Bf["pT_r"][s]])
                tmp, tmpr = next_tmp(Bf)
                P.op("act", lambda e, tmp=tmp, bg=bg: e.activation(tmp[:, :], bg[:, :], AF.Sigmoid),
                     reads=[bgr], writes=[tmpr])
                P.op("dve", lambda e, tmp=tmp, bp=bp: e.tensor_tensor(tmp[:, :], tmp[:, :], bp[:, :], ALU.mult),
                     reads=[tmpr, bpr], writes=[tmpr])
                xsl = Bf["xs"][s][:, b * 512:(b + 1) * 512]
                P.op("dve", lambda e, xsl=xsl, tmp=tmp: e.tensor_tensor(xsl, xsl, tmp[:, :], ALU.add),
                     reads=[tmpr], writes=[Bf["xs_r"][s]])

    def ln_and_xT(Bf, vec2):
        ln_stats([(Bf["xs"][s][:, :], Bf["xs_r"][s]) for s in range(4)], True)
        for s in range(4):
            ln_apply(s, Bf["xs"][s][:, :], Bf["xs_r"][s])
        for s in range(4):
            make_xT(Bf, s, ev1="act")

    def load_gb_pair(vec2):
        P.op("sp", lambda e: e.dma_start(out=gb[:, 0, :], in_=vec2[0, :].partition_broadcast(128)),
             writes=[gb_r], dma=True)
        if gb_extra[0] is None:
            gb_extra[0] = Res("gb2")
        P.op("sp", lambda e: e.dma_start(out=gb[:, 1, :], in_=vec2[1, :].partition_broadcast(128)),
             writes=[gb_extra[0]], dma=True)

    cf = sb("cf", [128, 256], F32, at=ring_off0)
    wsf = sb("wsf", [128, 16, 128], F32, at=ring_off0 + 1024)
    wsb = sb("wsb", [128, 16, 128], BF16, at=ring_off0 + 1024 + 8192)
    r0_ = ring_res[0]
    const_r = Res("const")
    P.op("sp", lambda e: e.dma_start(out=cf[:, 0:128], in_=cident), writes=[r0_], dma=True)
    P.op("pool", lambda e: e.tensor_copy(ident[:, :], cf[:, 0:128]), reads=[r0_], writes=[const_r])
    P.op("sp", lambda e: e.dma_start(out=cf[:, :], in_=cmask), writes=[r0_], dma=True)
    P.op("pool", lambda e: e.tensor_copy(mask2[:, :], cf[:, :]), reads=[r0_], writes=[const_r])
    P.op("pool", lambda e: e.memset(ones[:, :], 1.0), writes=[const_r])
    P.op("pool", lambda e: e.memset(epsc[:, :], EPS), writes=[const_r])
    P.op("sp", lambda e: e.dma_start(out=kb[:, :], in_=kbin), writes=[Res()], dma=True)
    P.op("sp", lambda e: e.dma_start(out=invf[:, :], in_=cinvf.partition_broadcast(128)), writes=[Res()], dma=True)
    P.op("sp", lambda e: e.dma_start(out=bsb[:, :, :].rearrange("p g q -> p (g q)"), in_=b_s.partition_broadcast(128)),
         writes=[Res()], dma=True)
    P.op("sp", lambda e: e.dma_start(out=wsf[:, :, :], in_=w_s.rearrange("g p q -> p g q")), writes=[r0_], dma=True)
    P.op("pool", lambda e: e.tensor_copy(wsb[:, :, :], wsf[:, :, :]), reads=[r0_], writes=[r0_])
    for hf in range(2):
        bk, br = nbank()
        bv = bk[:, 0:512].bitcast(BF16)
        tr_group(br, [(bv[:, k * 128:(k + 1) * 128], wsb[:, hf * 8 + k, :]) for k in range(8)], [r0_, const_r])
        P.op("dve", lambda e, hf=hf, bv=bv: e.tensor_copy(wsT[:, hf * 8:(hf + 1) * 8, :], bv.rearrange("p (k t) -> p k t", t=128)),
             reads=[br], writes=[const_r])
    P.barrier()

    def cast_w(k):
        nrow = Wf[k].shape[0]
        for kc in range(nrow // 128):
            P.op("pool", lambda e, k=k, kc=kc: e.dma_start(out=Wb[k][kc * 128:(kc + 1) * 128, :], in_=Wf[k][kc * 128:(kc + 1) * 128, :]),
                 writes=[Wres[k][kc]], dma=True)
    cast_w("w_in")
    cast_w("w_o_a")
    for k in ("gu0", "dn0", "gt0", "pj0", "w_qkv", "w_o_b", "gu1", "dn1", "gt1", "pj1"):
        cast_w(k)

    out_ops = []

    def load_x(ti):
        Bf = B1
        for s in range(4):
            xs = Bf["xs"][s]
            src = xin[ti * 512 + s * 128: ti * 512 + (s + 1) * 128, :]
            P.op("sp", lambda e, xs=xs, src=src: e.dma_start(out=xs[:, :], in_=src), writes=[Bf["xs_r"][s]], dma=True)

    def load_pos(ti):
        Bf = B1
        for s in range(4):
            P.op("sp", lambda e, s=s: e.dma_start(out=Bf["posb"][:, s, :], in_=posin[ti * 512 + s * 128: ti * 512 + (s + 1) * 128, :]),
                 writes=[Bf["posb_r"][s]], dma=True)

    def phase1_tile(ti):
        Bf = B1
        e0 = TILE_E0[ti]
        e0c = TILE_E0C[ti]
        own = TILE_OWN[ti]
        xT, xTr = Bf["xT"], Bf["xT_r"]
        scrA, scrAr = Bf["scrA"], Bf["scrA_r"]

        def dump():
            for s in range(4):
                xs = Bf["xs"][s]
                dst = x3s[own * 512 + s * 128: own * 512 + (s + 1) * 128, :]
                P.op("pool", lambda e, xs=xs, dst=dst: e.dma_start(out=dst, in_=xs[:, :]),
                     reads=[Bf["xs_r"][s]], writes=[], dma=True)
        if ti == tile_list[0]:
            load_x(ti)
        for s in range(4):
            make_xT(Bf, s)
        load_pos(ti)
        for s in range(4):
            ang = Bf["ang"]
            rot = Bf["rot"]
            P.op("dve", lambda e, s=s: e.tensor_scalar(ang[:, 0, :], invf[:, :], Bf["posb"][:, s, :], None, ALU.mult),
                 reads=[Bf["posb_r"][s]], writes=[Bf["ang_r"]])
            MAGIC = 12582912.0
            C1 = 6.28125
            C2 = 2 * math.pi - 6.28125
            ar = Bf["ang_r"]
            P.op("dve", lambda e: e.tensor_scalar(ang[:, 1, :], ang[:, 0, :], math.pi / 2, None, ALU.add), reads=[ar], writes=[ar])
            for q_ in range(2):
                P.op("dve", lambda e, q_=q_: e.tensor_scalar(ang[:, 2 + q_, :], ang[:, q_, :], 1.0 / (2 * math.pi), MAGIC, ALU.mult, ALU.add),
                     reads=[ar], writes=[ar])
                P.op("dve", lambda e, q_=q_: e.tensor_scalar(ang[:, 2 + q_, :], ang[:, 2 + q_, :], MAGIC, None, ALU.subtract),
                     reads=[ar], writes=[ar])
                P.op("dve", lambda e, q_=q_: e.scalar_tensor_tensor(ang[:, q_, :], ang[:, 2 + q_, :], -C1, ang[:, q_, :], ALU.mult, ALU.add),
                     reads=[ar], writes=[ar])
                P.op("dve", lambda e, q_=q_: e.scalar_tensor_tensor(ang[:, q_, :], ang[:, 2 + q_, :], -C2, ang[:, q_, :], ALU.mult, ALU.add),
                     reads=[ar], writes=[ar])
                P.op("dve", lambda e, q_=q_: e.tensor_scalar(ang[:, q_, :], ang[:, q_, :], -3.1415925, 3.1415925, ALU.max, ALU.min),
                     reads=[ar], writes=[ar])
            P.op("act", lambda e: e.activation(ang[:, 0:2, :], ang[:, 0:2, :], AF.Sin), reads=[Bf["ang_r"]], writes=[Bf["ang_r"]])
            P.op("dve", lambda e, s=s: e.tensor_scalar(rot[:, s, 0, :], ang[:, 1, :], QSCALE, None, ALU.mult),
                 reads=[Bf["ang_r"]], writes=[Bf["rot_r"][s]])
            P.op("dve", lambda e, s=s: e.tensor_scalar(rot[:, s, 1, :], ang[:, 0, :], QSCALE, None, ALU.mult),
                 reads=[Bf["ang_r"]], writes=[Bf["rot_r"][s]])
            P.op("dve", lambda e, s=s: e.tensor_scalar(rot[:, s, 2, :], ang[:, 1, :], 1.0, None, ALU.mult),
                 reads=[Bf["ang_r"]], writes=[Bf["rot_r"][s]])
            P.op("dve", lambda e, s=s: e.tensor_scalar(rot[:, s, 3, :], ang[:, 0, :], 1.0, None, ALU.mult),
                 reads=[Bf["ang_r"]], writes=[Bf["rot_r"][s]])
        load_gb_pair(lnv)
        for b in range(4):
            sl, slr = load_w("w_in", 0, 16, 2048 + b * 512)
            for s in range(4):
                bk, br = nbank()
                mm_group(br, [(bk[:, :], xT[:, kc, s * 128:(s + 1) * 128], sl[:, kc, :], kc == 0, kc == 15)
                              for kc in range(16)], [slr, xTr[s]])
                tmp, tmpr = next_tmp(Bf)
                P.op("act", lambda e, tmp=tmp, bk=bk: e.activation(tmp[:, :], bk[:, :], AF.Gelu), reads=[br], writes=[tmpr])
                P.op("dve", lambda e, s=s, b=b, tmp=tmp: e.bn_stats(stat[:, s, b, :], tmp[:, :]),
                     reads=[tmpr], writes=[stat_rc[s][b]])
                vv = Bf["vv"][s]
                P.op("pool", lambda e, vv=vv, b=b, tmp=tmp: e.tensor_copy(vv[:, b * 512:(b + 1) * 512], tmp[:, :]),
                     reads=[tmpr], writes=[Bf["vv_r"][s]])
        ln_stats([(Bf["vv"][s][:, :], Bf["vv_r"][s]) for s in range(4)], True)
        for s in range(4):
            ln_apply(s, Bf["vv"][s][:, :], Bf["vv_r"][s])
        for blk in range(4):
            sl, slr = load_w("w_in", 0, 16, blk * 512)
            for jj in range(4):
                j = blk * 4 + jj
                bk, br = nbank()
                mm_group(br, [(bk[:, :], sl[:, kc, jj * 128:(jj + 1) * 128], xT[:, kc, :], kc == 0, kc == 15)
                              for kc in range(16)], [slr] + xTr)
                P.op("act", lambda e, j=j, bk=bk: e.activation(scrA[:, j, :], bk[:, :], AF.Gelu),
                     reads=[br], writes=[scrAr[j]])
        for g in range(16):
            bk, br = nbank()
            mm_group(br, [(bk[:, s * 128:(s + 1) * 128], Bf["vv"][s][:, g * 128:(g + 1) * 128], wsT[:, g, :], True, True)
                          for s in range(4)], Bf["vv_r"])
            tmp, tmpr = next_tmp(Bf)
            P.op("dve", lambda e, g=g, tmp=tmp, bk=bk: e.tensor_tensor(
                tmp[:, :].rearrange("p (s q) -> p s q", q=128), bk[:, :].rearrange("p (s q) -> p s q", q=128),
                bsb[:, g, :].unsqueeze(1).to_broadcast([128, 4, 128]), ALU.add), reads=[br], writes=[tmpr])
            P.op("dve", lambda e, g=g, tmp=tmp: e.tensor_tensor(scrA[:, g, :], tmp[:, :], scrA[:, g, :], ALU.mult),
                 reads=[tmpr], writes=[scrAr[g]])
        load_gb_pair(lnmix[0])
        for b in range(4):
            sl, slr = load_w("w_o_a", 0, 16, b * 512)
            for s in range(4):
                bk, br = nbank()
                mm_group(br, [(bk[:, :], scrA[:, g, s * 128:(s + 1) * 128], sl[:, g, :], g == 0, g == 15)
                              for g in range(16)], [slr] + scrAr[:16])
                xsl = Bf["xs"][s][:, b * 512:(b + 1) * 512]
                P.op("dve", lambda e, xsl=xsl, bk=bk: e.scalar_tensor_tensor(xsl, xsl, ALPHA, bk[:, :], ALU.mult, ALU.add),
                     reads=[br], writes=[Bf["xs_r"][s]])
                P.op("dve", lambda e, xsl=xsl, s=s, b=b: e.bn_stats(stat[:, s, b, :], xsl),
                     reads=[Bf["xs_r"][s]], writes=[stat_rc[s][b]])
        ln_and_xT(Bf, lnmix[0])
        ffn(Bf, 0)
        ln_and_xT(Bf, lnffn[0])
        ple(Bf, 0, p0in, ti * 512)
        if own >= 0:
            for s in range(4):
                xs = Bf["xs"][s]
                dst = x3s[own * 512 + s * 128: own * 512 + (s + 1) * 128, :]
                P.op("pool", lambda e, xs=xs, dst=dst: e.dma_start(out=dst, in_=xs[:, :]),
                     reads=[Bf["xs_r"][s]], writes=[], dma=True)
        for s in range(4):
            make_xT(Bf, s)
        nxt = tile_list[tile_list.index(ti) + 1] if tile_list.index(ti) + 1 < len(tile_list) else None
        pref = [nxt is not None]
        nblk_done = [0]
        pending = [None]
        for g in range(3):
            for t in range(3):
                if t == 0 and own < 0:
                    continue
                for half in range(2):
                    c0 = g * 3072 + t * 1024 + half * 512
                    sl, slr = load_w("w_qkv", 0, 16, c0)
                    nblk_done[0] += 1
                    if pref[0] and nblk_done[0] == 3:
                        pref[0] = False
                        load_x(nxt)
                    if t < 2:
                        qi = Bf["cnt"]["qk"] % 2
                        Bf["cnt"]["qk"] += 1
                        qk, qkr = Bf["qktm"][qi], Bf["qktm_r"][qi]
                    for s in range(4):
                        bk, br = nbank()
                        mm_group(br, [(bk[:, :], xT[:, kc, s * 128:(s + 1) * 128], sl[:, kc, :], kc == 0, kc == 15)
                                      for kc in range(16)], [slr, xTr[s]])
                        if t == 2:
                            vi = Bf["cnt"]["vst"] % 2
                            Bf["cnt"]["vst"] += 1
                            vst, vstr = Bf["vst"][vi], Bf["vst_r"][vi]
                            P.op("act", lambda e, vst=vst, bk=bk: e.copy(vst[:, :], bk[:, :]), reads=[br], writes=[vstr])
                            for eb in ([e0] if e0c < 0 else [e0, e0c]):
                                dst = Vs[eb + s * 128: eb + (s + 1) * 128, g * 1024 + half * 512: g * 1024 + (half + 1) * 512]
                                P.op("pool", lambda e, vst=vst, dst=dst: e.dma_start(out=dst, in_=vst[:, :]),
                                     reads=[vstr], writes=[], dma=True)
                        else:
                            psv = bk[:, :].rearrange("p (h d) -> p h d", d=128)
                            qv = qk[:, s, :].rearrange("p (h d) -> p h d", d=128)
                            rot = Bf["rot"]
                            rtmp = Bf["rtmp"]
                            ci, si = (0, 1) if t == 0 else (2, 3)
                            cosb = rot[:, s, ci, :].unsqueeze(1).to_broadcast([128, 4, 16])
                            sinb = rot[:, s, si, :].unsqueeze(1).to_broadcast([128, 4, 16])
                            rr = [br, Bf["rot_r"][s]]
                            P.op("dve", lambda e, qv=qv, psv=psv, t=t: e.tensor_scalar(
                                qv[:, :, 32:128], psv[:, :, 32:128], (QSCALE if t == 0 else 1.0), None, ALU.mult),
                                reads=[br], writes=[qkr])
                            P.op("dve", lambda e, psv=psv, cosb=cosb: e.tensor_tensor(rtmp[:, 0, :, :], psv[:, :, 0:16], cosb, ALU.mult),
                                 reads=rr, writes=[Bf["rtmp_r"]])
                            P.op("dve", lambda e, psv=psv, sinb=sinb: e.tensor_tensor(rtmp[:, 1, :, :], psv[:, :, 16:32], sinb, ALU.mult),
                                 reads=rr, writes=[Bf["rtmp_r"]])
                            P.op("dve", lambda e, qv=qv: e.tensor_tensor(qv[:, :, 0:16], rtmp[:, 0, :, :], rtmp[:, 1, :, :], ALU.subtract),
                                 reads=[Bf["rtmp_r"]], writes=[qkr])
                            P.op("dve", lambda e, psv=psv, sinb=sinb: e.tensor_tensor(rtmp[:, 0, :, :], psv[:, :, 0:16], sinb, ALU.mult),
                                 reads=rr, writes=[Bf["rtmp_r"]])
                            P.op("dve", lambda e, psv=psv, cosb=cosb: e.tensor_tensor(rtmp[:, 1, :, :], psv[:, :, 16:32], cosb, ALU.mult),
                                 reads=rr, writes=[Bf["rtmp_r"]])
                            P.op("dve", lambda e, qv=qv: e.tensor_tensor(qv[:, :, 16:32], rtmp[:, 0, :, :], rtmp[:, 1, :, :], ALU.add),
                                 reads=[Bf["rtmp_r"]], writes=[qkr])
                    if pending[0] is not None:
                        pending[0]()
                        pending[0] = None
                    if t < 2:
                        def do_tr(qk=qk, qkr=qkr, g=g, t=t, half=half):
                            si_ = Bf["cnt"]["qst"] % 2
                            Bf["cnt"]["qst"] += 1
                            qst, qstr = Bf["qst"][si_], Bf["qst_r"][si_]
                            for hp2 in range(2):
                                bk, br = nbank()
                                bv = bk[:, 0:512].bitcast(BF16)
                                items = []
                                for hl in range(2):
                                    hh = hp2 * 2 + hl
                                    for s in range(4):
                                        items.append((bv[:, hl * 512 + s * 128: hl * 512 + (s + 1) * 128],
                                                      qk[:, s, hh * 128:(hh + 1) * 128]))
                                tr_group(br, items, [qkr])
                                dst = qst[:, hp2 * 2:(hp2 + 1) * 2, :]
                                srcv = bv.rearrange("p (h t) -> p h t", t=512)
                                if hp2 == 0:
                                    P.op("act", lambda e, dst=dst, srcv=srcv: e.copy(dst, srcv), reads=[br], writes=[qstr])
                                else:
                                    P.op("dve", lambda e, dst=dst, srcv=srcv: e.tensor_copy(dst, srcv), reads=[br], writes=[qstr])
                            gh0 = g * 8 + half * 4
                            T_ = QT if t == 0 else KT
                            for eb in ([e0] if e0c < 0 else [e0, e0c]):
                                dst = T_[gh0:gh0 + 4, :, eb:eb + 512].rearrange("h d t -> d h t")
                                P.op("pool", lambda e, dst=dst, qst=qst: e.dma_start(out=dst, in_=qst[:, :, :]),
                                     reads=[qstr], writes=[], dma=True)
                        pending[0] = do_tr
        if pending[0] is not None:
            pending[0]()
            pending[0] = None

    for ti in tile_list:
        phase1_tile(ti)

    def attention(st):
        e0 = ST_E0[st]
        DEPTH = 2
        its = []
        qkc = [0]
        vc = [0]
        pcnt = [0]
        for hp in range(4):
            for g in range(3):
                dil = DILS[g]
                nkt = NKT[g]
                nq = 16 // dil
                W = 2048 + 128 * dil
                vstate = {}

                def load_v(hp=hp, g=g, dil=dil, nkt=nkt, vstate=vstate):
                    vi = vc[0] % 2
                    vc[0] += 1
                    vw, vwr = vwin[vi], vwin_r[vi]
                    vdeps = []
                    for r in range(dil):
                        es = e0 - 64 * dil + r
                        src = Vs[es: es + dil * (128 * nkt - 1) + 1: dil, g * 1024 + hp * 256: g * 1024 + (hp + 1) * 256]
                        src = src.rearrange("(j c) w -> c j w", c=128)
                        dst = vw[:, r * nkt:(r + 1) * nkt, :]
                        vdeps.append(P.op("sp", lambda e, dst=dst, src=src: e.dma_start(out=dst, in_=src),
                                          writes=[vwr] if r == 0 else (), dma=True))
                    vstate.update(vw=vw, vwr=vwr, vdeps=vdeps)
                for hl in range(2):
                    h = hp * 2 + hl
                    gh = g * 8 + h
                    qstate = {}

                    def load_qk(gh=gh, W=W, dil=dil, qstate=qstate):
                        qi = qkc[0] % 2
                        qkc[0] += 1
                        qw, kw, qkr, kwr = qwin[qi], kwin[qi], qk_r[qi], kw_r[qi]
                        P.op("sp", lambda e: e.dma_start(out=qw[:, :], in_=QT[gh, :, e0:e0 + 2048]), writes=[qkr], dma=True)
                        P.op("sp", lambda e: e.dma_start(out=kw[:, 0:W], in_=KT[gh, :, e0 - 64 * dil: e0 - 64 * dil + W]),
                             writes=[kwr], dma=True)
                        qstate.update(qw=qw, kw=kw, qkr=qkr, kwr=kwr)
                    for r in range(dil):
                        for n in range(nq):
                            pre = []
                            if hl == 0 and r == 0 and n == 0:
                                pre.append(load_v)
                            if r == 0 and n == 0:
                                pre.append(load_qk)
                            post = []
                            last_of_hp = (g == 2 and hl == 1 and r == dil - 1 and n == nq - 1)
                            its.append(dict(pre=pre, hp=hp, g=g, dil=dil, nkt=nkt, hl=hl, r=r, n=n, vstate=vstate,
                                            qstate=qstate, last_of_hp=last_of_hp))

        def stageA(it):
            for f in it["pre"]:
                f()
            g, dil, nkt, r, n = it["g"], it["dil"], it["nkt"], it["r"], it["n"]
            qs = it["qstate"]
            qw, kw = qs["qw"], qs["kw"]
            bk, br = nbank()
            it["bk"], it["br"] = bk, br
            if dil > 1:
                q_ap = qw[:, r + dil * 128 * n: r + dil * 128 * n + dil * 127 + 1: dil]
            else:
                q_ap = qw[:, 128 * n:128 * (n + 1)]

            def kap(j):
                if dil > 1:
                    return kw[:, r + dil * 128 * j: r + dil * 128 * j + dil * 127 + 1: dil]
                return kw[:, 128 * j:128 * (j + 1)]
            mm_group(br, [(bk[:, 0:128], kap(n), q_ap, True, True),
                          (bk[:, 128:256], kap(n + 1), q_ap, True, True)], [qs["qkr"], qs["kwr"]])
            pi = pcnt[0] % 4
            pcnt[0] += 1
            pf, pfr, pm, pmr = ptf[pi], ptf_r[pi], ptm[pi], ptm_r[pi]
            it["pm"], it["pmr"] = pm, pmr
            col = st * NKB + KOFF[g] + r * nkt + n
            P.op("act", lambda e: e.activation(pf[:, 0:128], bk[:, 0:128], AF.Exp, bias=kb[:, col:col + 1]),
                 reads=[br], writes=[pfr])
            P.op("act", lambda e: e.activation(pf[:, 128:256], bk[:, 128:256], AF.Exp, bias=kb[:, col + 1:col + 2]),
                 reads=[br], writes=[pfr])
            P.op("dve", lambda e: e.tensor_tensor(pm[:, :], pf[:, :], mask2[:, :], ALU.mult), reads=[pfr], writes=[pmr])

        def stageB(it):
            g, dil, nkt, r, n, hl, hp = it["g"], it["dil"], it["nkt"], it["r"], it["n"], it["hl"], it["hp"]
            vs = it["vstate"]
            vw = vs["vw"]
            bk, br, pm, pmr = it["bk"], it["br"], it["pm"], it["pmr"]
            vA = vw[:, r * nkt + n, hl * 128:(hl + 1) * 128]
            vB = vw[:, r * nkt + n + 1, hl * 128:(hl + 1) * 128]
            mm_group(br, [(bk[:, 256:384], vA, pm[:, 0:128], True, False),
                          (bk[:, 256:384], vB, pm[:, 128:256], False, True),
                          (bk[:, 384:512], ones[:, :], pm[:, 0:128], True, False),
                          (bk[:, 384:512], ones[:, :], pm[:, 128:256], False, True)], [pmr, vs["vwr"]], extra=vs["vdeps"])
            t0 = r + dil * 128 * n
            if dil > 1:
                acc_ap = acc[:, hl, :, t0: t0 + dil * 127 + 1: dil]
            else:
                acc_ap = acc[:, hl, :, t0:t0 + 128]
            pso = bk[:, 256:512].rearrange("p (a q) -> p a q", q=128)
            if g == 0:
                P.op("dve", lambda e: e.tensor_copy(acc_ap, pso), reads=[br], writes=[acc_r[hl]])
            else:
                P.op("dve", lambda e: e.tensor_tensor(acc_ap, acc_ap, pso, ALU.add), reads=[br], writes=[acc_r[hl]])
            if it["last_of_hp"]:
                for hl2 in range(2):
                    h = hp * 2 + hl2
                    P.op("dve", lambda e, hl2=hl2: e.reciprocal(acc[:, hl2, 1, :], acc[:, hl2, 1, :]),
                         reads=[acc_r[hl2]], writes=[acc_r[hl2]])
                    P.op("dve", lambda e, hl2=hl2, h=h: e.tensor_tensor(attnT[:, h, :], acc[:, hl2, 0, :], acc[:, hl2, 1, :], ALU.mult),
                         reads=[acc_r[hl2]], writes=[attnT_r[h]])

        n_it = len(its)
        for i in range(min(DEPTH, n_it)):
            stageA(its[i])
        for i in range(n_it):
            if i + DEPTH < n_it:
                stageA(its[i + DEPTH])
            stageB(its[i])

    def phase2_tile(st, i):
        Bf = B2
        own = st * 4 + i
        toff = i * 512
        for s in range(4):
            xs = Bf["xs"][s]
            src = x3s[own * 512 + s * 128: own * 512 + (s + 1) * 128, :]
            P.op("sp", lambda e, xs=xs, src=src: e.dma_start(out=xs[:, :], in_=src), writes=[Bf["xs_r"][s]], dma=True)
        load_gb_pair(lnmix[1])
        for b in range(4):
            sl, slr = load_w("w_o_b", 0, 8, b * 512)
            for s in range(4):
                bk, br = nbank()
                mm_group(br, [(bk[:, :], attnT[:, h, toff + s * 128: toff + (s + 1) * 128], sl[:, h, :], h == 0, h == 7)
                              for h in range(8)], [slr] + attnT_r)
                xsl = Bf["xs"][s][:, b * 512:(b + 1) * 512]
                P.op("dve", lambda e, xsl=xsl, bk=bk: e.scalar_tensor_tensor(xsl, xsl, ALPHA, bk[:, :], ALU.mult, ALU.add),
                     reads=[br], writes=[Bf["xs_r"][s]])
                P.op("dve", lambda e, xsl=xsl, s=s, b=b: e.bn_stats(stat[:, s, b, :], xsl),
                     reads=[Bf["xs_r"][s]], writes=[stat_rc[s][b]])
        ln_and_xT(Bf, lnmix[1])
        ffn(Bf, 1)
        ln_and_xT(Bf, lnffn[1])
        ple(Bf, 1, p1in, own * 512)
        for s in range(4):
            xs = Bf["xs"][s]
            dst = yout[own * 512 + s * 128: own * 512 + (s + 1) * 128, :]
            o = P.op("pool", lambda e, xs=xs, dst=dst: e.dma_start(out=dst, in_=xs[:, :]),
                     reads=[Bf["xs_r"][s]], writes=[], dma=True)
            out_ops.append(o)

    nslot_cur[0] = 3
    slot_cnt[0] = 0
    for st in st_list:
        P.barrier()
        attention(st)
        P.barrier()
        for i in range(4):
            phase2_tile(st, i)
    P.barrier()
    P.op("sp", None)

    emit(nc, P)
    return nc


def emit(nc, P):
    for e in ENGS:
        for o in P.q[e]:
            for d in o.deps:
                if not d.dma:
                    d.signal = True
    nsig = {}
    for e in ENGS:
        c = 0
        for o in P.q[e]:
            if o.signal and not o.dma:
                c += 1
                o.ticket = c
        nsig[e] = c
    NDS = {"sp": 20, "pool": 28}
    for e in ("sp", "pool"):
        K = NDS[e]
        for i, o in enumerate(P.dma_ops[e]):
            o.dsem = i % K
            o.dval = 16 * (i // K + 1)
            o.prev_dval = 16 * (i // K)

    from contextlib import ExitStack
    with ExitStack() as es:
        csem = {}
        for e in ENGS:
            nep = (nsig[e] + EPOCH - 1) // EPOCH
            csem[e] = [es.enter_context(nc.semaphore(f"c_{e}_{k}")) for k in range(max(nep, 1))]
        dsem = {e: [es.enter_context(nc.semaphore(f"d_{e}_{k}")) for k in range(NDS[e])] for e in ("sp", "pool")}
        block = es.enter_context(nc.Block())

        def run_engine(ename, eng):
            seen = {}

            def wait(key, sem, val):
                if seen.get(key, 0) >= val:
                    return
                seen[key] = val
                eng.wait_ge(sem, val)

            for o in P.q[ename]:
                cmax = {}
                for d in o.deps:
                    if d.dma:
                        wait(("d", d.eng, d.dsem), dsem[d.eng][d.dsem], d.dval)
                    else:
                        ep = (d.ticket - 1) // EPOCH
                        v = d.ticket - ep * EPOCH
                        k = (d.eng, ep)
                        if cmax.get(k, 0) < v:
                            cmax[k] = v
                for (de, ep), v in cmax.items():
                    wait(("c", de, ep), csem[de][ep], v)
                if o.dma and o.prev_dval > 0:
                    wait(("d", o.eng, o.dsem), dsem[o.eng][o.dsem], o.prev_dval)
                if o.fn is None:
                    continue
                ins = o.fn(eng)
                if o.dma:
                    ins.then_inc(dsem[o.eng][o.dsem], 16)
                elif o.signal:
                    ep = (o.ticket - 1) // EPOCH
                    ins.then_inc(csem[o.eng][ep], 1)

        @block.tensor
        def _(eng):
            run_engine("pe", eng)

        @block.scalar
        def _(eng):
            run_engine("act", eng)

        @block.vector
        def _(eng):
            run_engine("dve", eng)

        @block.gpsimd
        def _(eng):
            run_engine("pool", eng)

        @block.sync
        def _(eng):
            run_engine("sp", eng)


def host_inputs(inputs, cores=range(NCORE)):
    xp = np.asarray(inputs["x_prompt"], np.float32)
    xs = np.asarray(inputs["x_sample"], np.float32)
    pp = np.asarray(inputs["p_prompt"], np.float32)
    ps = np.asarray(inputs["p_sample"], np.float32)
    f = lambda k: np.ascontiguousarray(np.asarray(inputs[k], np.float32))
    shared = {
        "w_in": f("w_in_a")[0], "w_o_a": f("w_o_a")[0], "w_qkv": f("w_qkv_b")[0], "w_o_b": f("w_o_b")[0],
        "gu0": f("w_ffn_gu")[0], "gu1": f("w_ffn_gu")[1], "dn0": f("w_ffn_down")[0], "dn1": f("w_ffn_down")[1],
        "gt0": f("w_ple_gate")[0], "gt1": f("w_ple_gate")[1], "pj0": f("w_ple_proj")[0], "pj1": f("w_ple_proj")[1],
        "w_s": f("w_s_a")[0], "b_s": f("b_s_a")[0].reshape(2048),
        "ln_v": np.stack([f("ln_v_g")[0], f("ln_v_b")[0]]),
        "ln_mix": np.stack([f("ln_mix_g"), f("ln_mix_b")], axis=1),
        "ln_ffn": np.stack([f("ln_ffn_g"), f("ln_ffn_b")], axis=1),
        "cident": np.eye(128, dtype=np.float32),
    }
    c = np.arange(128)[:, None]
    a = np.arange(128)[None, :]
    shared["cmask"] = np.concatenate([(a <= c), (a >= c)], axis=1).astype(np.float32)
    shared["cinvf"] = (500000.0 ** (-np.arange(0, 32, 2, dtype=np.float32) / 32)).astype(np.float32)
    shared = {k: np.ascontiguousarray(v) for k, v in shared.items()}
    maps = []
    for core in cores:
        xin = np.zeros((NLOC, D), np.float32)
        p0 = np.zeros((NLOC, 256), np.float32)
        pos = np.zeros((NLOC, 1), np.float32)
        if core < 4:
            for k in range(4):
                xin[2048 * k:2048 * (k + 1)] = xp[4 * core + k]
                p0[2048 * k:2048 * (k + 1)] = pp[0, 4 * core + k]
                pos[2048 * k:2048 * (k + 1), 0] = np.arange(2048)
            p1 = np.concatenate([pp[1, 4 * core + k] for k in range(4)], axis=0)
        else:
            sbi, seg = (core - 4) // 2, (core - 4) % 2
            o0 = seg * 8192
            h0 = 8192 if seg == 0 else 7168
            xin[0:NOWN] = xs[sbi, o0:o0 + 8192]
            xin[NOWN:] = xs[sbi, h0:h0 + 1024]
            p0[0:NOWN] = ps[0, sbi, o0:o0 + 8192]
            p0[NOWN:] = ps[0, sbi, h0:h0 + 1024]
            pos[0:NOWN, 0] = np.arange(o0, o0 + 8192)
            pos[NOWN:, 0] = np.arange(h0, h0 + 1024)
            p1 = ps[1, sbi, o0:o0 + 8192]
        kbm = np.zeros((128, 4 * NKB), np.float32)
        for st in range(4):
            e0 = ST_E0[st]
            for g in range(3):
                dil, nkt = DILS[g], NKT[g]
                for r in range(dil):
                    for j in range(nkt):
                        e = e0 - 64 * dil + r + dil * (128 * j + np.arange(128))
                        is_own = (e >= 1024) & (e < 1024 + NOWN)
                        if core < 4:
                            valid = is_own & ((e - 1024) // 2048 == st)
                        else:
                            seg = (core - 4) % 2
                            valid = is_own | ((e >= 1024 + NOWN) if seg == 0 else (e < 1024))
                        kbm[:, st * NKB + KOFF[g] + r * nkt + j] = np.where(valid, 0.0, NEG)
        m = dict(shared)
        m.update({"xin": xin, "p0in": p0, "p1in": np.ascontiguousarray(p1), "posin": pos, "kbin": kbm})
        maps.append(m)
    return maps


def assemble(results):
    yp = np.zeros((16, 2048, D), np.float32)
    ys = np.zeros((2, 16384, D), np.float32)
    for core in range(NCORE):
        y = np.asarray(results[core]["yout"], np.float32)
        if core < 4:
            for k in range(4):
                yp[4 * core + k] = y[2048 * k:2048 * (k + 1)]
        else:
            sbi, seg = (core - 4) // 2, (core - 4) % 2
            ys[sbi, seg * 8192:(seg + 1) * 8192] = y
    return yp, ys


def kernel(**inputs):
    maps = host_inputs(inputs)
    nc = build()
    res = run_bass_kernel_spmd(nc, maps, core_ids=list(range(NCORE)))
    return assemble(res.results)
```
